# Optimizing a Trainium2 kernel written in Bass

```python
import math
import jax, jax.numpy as jnp
from jax import lax
import numpy as np

D_MODEL = 1024
BATCH = 2
SEQ = 16384
DEPTH = 4

MEM_LEN = 256
HEAD_DIM = 64
DIFF_HEADS = 4
DIFF_QK = 32
DIFF_V = 2 * DIFF_QK
MLA_HEADS = 4
MLA_Q_RANK = 256
MLA_KV_RANK = 128
MLA_NOPE = 64
MLA_ROPE = 32
MLA_V = 64
SWA_Q_HEADS = 8
SWA_KV_HEADS = 2
SWA_GROUP = SWA_Q_HEADS // SWA_KV_HEADS
SWA_WINDOW = 128
D_MIX = DIFF_HEADS * DIFF_V + MLA_HEADS * MLA_V + SWA_Q_HEADS * HEAD_DIM
IN_SIZES = [DIFF_HEADS * 2 * DIFF_QK, DIFF_HEADS * 2 * DIFF_QK, DIFF_HEADS * DIFF_V,
            MLA_Q_RANK, MLA_KV_RANK, MLA_ROPE,
            SWA_Q_HEADS * HEAD_DIM, SWA_KV_HEADS * HEAD_DIM, SWA_KV_HEADS * HEAD_DIM]
D_IN = sum(IN_SIZES)
IN_SPLITS = [int(v) for v in np.cumsum(IN_SIZES)[:-1]]
X_HEADS = 4
X_HEAD_DIM = D_MODEL // X_HEADS
D_FF = -(-8 * D_MODEL // (3 * 256)) * 256

N_ALIBI = DIFF_HEADS + SWA_Q_HEADS
QBLK = 128
ROPE_THETA = 10000.0
EPS = 1e-6

kernel_name = "hymba_style_hybrid_encoder"


def rmsnorm(x, g):
    xf = x.astype(jnp.float32)
    y = xf * lax.rsqrt(jnp.mean(xf * xf, axis=-1, keepdims=True) + EPS)
    return (y * g.astype(jnp.float32)).astype(x.dtype)


def rope(x, pos):
    half = x.shape[-1] // 2
    inv = ROPE_THETA ** (-jnp.arange(half, dtype=jnp.float32) / half)
    ang = pos.astype(jnp.float32)[..., None] * inv
    ang = ang.reshape(ang.shape[:2] + (1,) * (x.ndim - 3) + (half,))
    cos, sin = jnp.cos(ang), jnp.sin(ang)
    xf = x.astype(jnp.float32)
    x1, x2 = xf[..., :half], xf[..., half:]
    return jnp.concatenate([x1 * cos - x2 * sin, x1 * sin + x2 * cos], axis=-1).astype(x.dtype)


def alibi_slopes():
    return 2.0 ** (-8.0 * jnp.arange(1, N_ALIBI + 1, dtype=jnp.float32) / N_ALIBI)


def diff_attention(q, k, v, pos, slopes, lam):
    B, S, H, _ = q.shape
    nblk = S // QBLK
    scale = 1.0 / math.sqrt(DIFF_QK)
    qb = q.reshape(B, nblk, QBLK, H, 2, DIFF_QK).swapaxes(0, 1)
    pb = pos.reshape(B, nblk, QBLK).swapaxes(0, 1)
    k2 = k.reshape(B, S, H, 2, DIFF_QK)

    def one_block(args):
        qi, pi = args
        s = jnp.einsum('bqhmd,bshmd->bmhqs', qi, k2).astype(jnp.float32) * scale
        dist = jnp.abs(pi[:, :, None] - pos[:, None, :]).astype(jnp.float32)
        s = s - slopes[None, None, :, None, None] * dist[:, None, None]
        p = jax.nn.softmax(s, axis=-1)
        a = p[:, 0] - lam * p[:, 1]
        return jnp.einsum('bhqs,bshd->bqhd', a.astype(v.dtype), v)

    out = lax.map(one_block, (qb, pb))
    return out.swapaxes(0, 1).reshape(B, S, H, DIFF_V)


def mla_attention(q, k, v):
    B, S, H, D = q.shape
    nblk = S // QBLK
    scale = 1.0 / math.sqrt(D)
    qb = q.reshape(B, nblk, QBLK, H, D).swapaxes(0, 1)

    def one_block(qi):
        s = jnp.einsum('bqhd,bshd->bhqs', qi, k).astype(jnp.float32) * scale
        p = jax.nn.softmax(s, axis=-1)
        return jnp.einsum('bhqs,bshd->bqhd', p.astype(v.dtype), v)

    out = lax.map(one_block, qb)
    return out.swapaxes(0, 1).reshape(B, S, H * MLA_V)


def swa_attention(q, k, v, pos, slopes, sinks):
    B, S = q.shape[:2]
    W = SWA_WINDOW
    nblk = S // W
    scale = 1.0 / math.sqrt(HEAD_DIM)

    def band(t):
        pad = [(0, 0), (W, W)] + [(0, 0)] * (t.ndim - 2)
        tp = jnp.pad(t, pad).reshape((t.shape[0], nblk + 2, W) + t.shape[2:])
        return jnp.concatenate([tp[:, :-2], tp[:, 1:-1], tp[:, 2:]], axis=2)

    kb, vb, pkb = band(k), band(v), band(pos)
    valid = band(jnp.ones((1, S), dtype=bool))
    qg = q.reshape(B, nblk, W, SWA_KV_HEADS, SWA_GROUP, HEAD_DIM)
    s = jnp.einsum('bnqkgd,bnskd->bnkgqs', qg, kb).astype(jnp.float32) * scale
    pq = pos.reshape(B, nblk, W)
    dist = jnp.abs(pq[..., :, None] - pkb[..., None, :]).astype(jnp.float32)
    s = s - slopes.reshape(SWA_KV_HEADS, SWA_GROUP)[:, :, None, None] * dist[:, :, None, None]
    offs = jnp.arange(3 * W)[None, :] - W - jnp.arange(W)[:, None]
    mask = (jnp.abs(offs) <= W)[None, None] & valid[:, :, None, :]
    s = jnp.where(mask[:, :, None, None], s, jnp.finfo(jnp.float32).min)
    sink = sinks.astype(jnp.float32).reshape(SWA_KV_HEADS, SWA_GROUP)[:, :, None, None]
    m = jnp.maximum(jnp.max(s, axis=-1, keepdims=True), sink)
    e = jnp.exp(s - m)
    p = e / (jnp.sum(e, axis=-1, keepdims=True) + jnp.exp(sink - m))
    o = jnp.einsum('bnkgqs,bnskd->bnqkgd', p.astype(v.dtype), vb)
    return o.reshape(B, S, SWA_Q_HEADS * HEAD_DIM)


def cross_attention(h, mem_n, w_q, w_kv, w_o):
    B, S, _ = h.shape
    M = mem_n.shape[1]
    q = (h @ w_q).reshape(B, S, X_HEADS, X_HEAD_DIM)
    k, v = jnp.split(mem_n @ w_kv, 2, axis=-1)
    k = k.reshape(B, M, X_HEADS, X_HEAD_DIM)
    v = v.reshape(B, M, X_HEADS, X_HEAD_DIM)
    s = jnp.einsum('bqhd,bmhd->bhqm', q, k).astype(jnp.float32) / math.sqrt(X_HEAD_DIM)
    p = jax.nn.softmax(s, axis=-1)
    o = jnp.einsum('bhqm,bmhd->bqhd', p.astype(v.dtype), v).reshape(B, S, D_MODEL)
    return o @ w_o


def setup_inputs(seed: int = 0) -> dict:
    key = jax.random.key(seed)
    ks = jax.random.split(key, 32)
    f32 = jnp.float32

    def w(k, shape, fan_in):
        return jax.random.normal(k, shape, f32) * fan_in ** -0.5

    def gain(k, shape):
        return 1.0 + 0.02 * jax.random.normal(k, shape, f32)

    offset = jax.random.randint(ks[2], (BATCH, 1), 0, 1024, dtype=jnp.int32)
    positions = offset + jnp.arange(SEQ, dtype=jnp.int32)[None, :]
    return {
        "x": jax.random.normal(ks[0], (BATCH, SEQ, D_MODEL), f32),
        "mem": jax.random.normal(ks[1], (BATCH, MEM_LEN, D_MODEL), f32),
        "positions": positions,
        "g_mix_pre": gain(ks[3], (DEPTH, D_MODEL)),
        "g_mix_post": gain(ks[4], (DEPTH, D_MODEL)),
        "w_in": w(ks[5], (DEPTH, D_MODEL, D_IN), D_MODEL),
        "diff_lambda": 0.1 * jax.random.normal(ks[6], (DEPTH, 4, DIFF_QK), f32),
        "diff_head_g": gain(ks[7], (DEPTH, DIFF_V)),
        "mla_q_norm_g": gain(ks[8], (DEPTH, MLA_Q_RANK)),
        "mla_w_q_up": w(ks[9], (DEPTH, MLA_Q_RANK, MLA_HEADS * (MLA_NOPE + MLA_ROPE)), MLA_Q_RANK),
        "mla_kv_norm_g": gain(ks[10], (DEPTH, MLA_KV_RANK)),
        "mla_w_kv_up": w(ks[11], (DEPTH, MLA_KV_RANK, MLA_HEADS * (MLA_NOPE + MLA_V)), MLA_KV_RANK),
        "swa_sinks": 0.5 * jax.random.normal(ks[12], (DEPTH, SWA_Q_HEADS), f32),
        "w_out": w(ks[13], (DEPTH, D_MIX, D_MODEL), D_MIX),
        "g_x_pre": gain(ks[14], (DEPTH, D_MODEL)),
        "g_x_mem": gain(ks[15], (DEPTH, D_MODEL)),
        "g_x_post": gain(ks[16], (DEPTH, D_MODEL)),
        "w_xq": w(ks[17], (DEPTH, D_MODEL, D_MODEL), D_MODEL),
        "w_xkv": w(ks[18], (DEPTH, D_MODEL, 2 * D_MODEL), D_MODEL),
        "w_xo": w(ks[19], (DEPTH, D_MODEL, D_MODEL), D_MODEL),
        "g_ffn_pre": gain(ks[20], (DEPTH, D_MODEL)),
        "g_ffn_post": gain(ks[21], (DEPTH, D_MODEL)),
        "w_ffn_in": w(ks[22], (DEPTH, D_MODEL, 2 * D_FF), D_MODEL),
        "w_ffn_out": w(ks[23], (DEPTH, D_FF, D_MODEL), D_FF),
    }


def reference(x, mem, positions, g_mix_pre, g_mix_post, w_in, diff_lambda, diff_head_g,
              mla_q_norm_g, mla_w_q_up, mla_kv_norm_g, mla_w_kv_up, swa_sinks, w_out,
              g_x_pre, g_x_mem, g_x_post, w_xq, w_xkv, w_xo,
              g_ffn_pre, g_ffn_post, w_ffn_in, w_ffn_out):
    B, S, _ = x.shape
    slopes = alibi_slopes()
    swa_slopes = slopes[:SWA_Q_HEADS]
    diff_slopes = slopes[SWA_Q_HEADS:]

    for l in range(DEPTH):
        h = rmsnorm(x, g_mix_pre[l])
        (a_q, a_k, a_v, b_cq, b_ckv, b_kr, c_q, c_k, c_v) = jnp.split(h @ w_in[l], IN_SPLITS, axis=-1)

        lam_init = 0.8 - 0.6 * math.exp(-0.3 * l)
        lp = diff_lambda[l].astype(jnp.float32)
        lam = jnp.exp(jnp.sum(lp[0] * lp[1])) - jnp.exp(jnp.sum(lp[2] * lp[3])) + lam_init
        o_a = diff_attention(a_q.reshape(B, S, DIFF_HEADS, 2 * DIFF_QK),
                             a_k.reshape(B, S, DIFF_HEADS, 2 * DIFF_QK),
                             a_v.reshape(B, S, DIFF_HEADS, DIFF_V),
                             positions, diff_slopes, lam)
        o_a = (rmsnorm(o_a, diff_head_g[l]) * (1.0 - lam_init)).reshape(B, S, DIFF_HEADS * DIFF_V)

        qb = (rmsnorm(b_cq, mla_q_norm_g[l]) @ mla_w_q_up[l]).reshape(B, S, MLA_HEADS, MLA_NOPE + MLA_ROPE)
        qb = jnp.concatenate([qb[..., :MLA_NOPE], rope(qb[..., MLA_NOPE:], positions)], axis=-1)
        kvb = (rmsnorm(b_ckv, mla_kv_norm_g[l]) @ mla_w_kv_up[l]).reshape(B, S, MLA_HEADS, MLA_NOPE + MLA_V)
        k_rope = jnp.broadcast_to(rope(b_kr, positions)[:, :, None, :], (B, S, MLA_HEADS, MLA_ROPE))
        kb = jnp.concatenate([kvb[..., :MLA_NOPE], k_rope], axis=-1)
        o_b = mla_attention(qb, kb, kvb[..., MLA_NOPE:])

        o_c = swa_attention(c_q.reshape(B, S, SWA_Q_HEADS, HEAD_DIM),
                            c_k.reshape(B, S, SWA_KV_HEADS, HEAD_DIM),
                            c_v.reshape(B, S, SWA_KV_HEADS, HEAD_DIM),
                            positions, swa_slopes, swa_sinks[l])

        mix = jnp.concatenate([o_a, o_b, o_c], axis=-1) @ w_out[l]
        x = x + rmsnorm(mix, g_mix_post[l])

        xo = cross_attention(rmsnorm(x, g_x_pre[l]), rmsnorm(mem, g_x_mem[l]), w_xq[l], w_xkv[l], w_xo[l])
        x = x + rmsnorm(xo, g_x_post[l])

        gate, up = jnp.split(rmsnorm(x, g_ffn_pre[l]) @ w_ffn_in[l], 2, axis=-1)
        f = (jax.nn.silu(gate) * up) @ w_ffn_out[l]
        x = x + rmsnorm(f, g_ffn_post[l])

    return x
```

```python
import math
from contextlib import ExitStack
import numpy as np
import ml_dtypes
import concourse.bass as bass
import concourse.mybir as mybir
from concourse.bass_utils import run_bass_kernel_spmd

F32 = mybir.dt.float32
BF16 = mybir.dt.bfloat16
I32 = mybir.dt.int32
AF = mybir.ActivationFunctionType
ALU = mybir.AluOpType

D = 1024
DEPTH = 4
D_IN = 1952
D_FF = 2816
MEM = 256
EPS = 1e-6
NCORES = 8
O_AQ, O_AK, O_AV, O_CQ, O_CKV, O_KR, O_SQ, O_SK, O_SV = 0, 256, 512, 768, 1024, 1152, 1184, 1696, 1824
N_ALIBI = 12
SLOPES = [float(np.float32(2.0) ** np.float32(-8.0 * i / N_ALIBI)) for i in range(1, N_ALIBI + 1)]
SWA_SLOPES = SLOPES[:8]
DIFF_SLOPES = SLOPES[8:]
HEAD_SLOPES = DIFF_SLOPES + SWA_SLOPES
NAUG = 9
TWO_PI = 2.0 * math.pi


class Res:
    __slots__ = ("name", "last_w", "readers", "semkey")

    def __init__(self, name, semkey=None):
        self.name = name
        self.last_w = None
        self.readers = []
        self.semkey = semkey or name


class _Rec:
    def __init__(self):
        self.call = None

    def __getattr__(self, name):
        def f(*a, **k):
            self.call = (name, a, k)
            return self
        return f


class Prog:
    ENG = ("pe", "act", "dve", "pool", "sp")

    def __init__(self, nc):
        self.nc = nc
        self.q = {e: [] for e in self.ENG}
        self.cnt = {e: 0 for e in self.ENG}
        self.waited = {e: {} for e in self.ENG}
        self.dma_cnt = {}
        self.dma_last = {}
        self.stack = ExitStack()
        self.sems = {}
        self.nres = 0

    def res(self, name=None, semkey=None):
        self.nres += 1
        return Res(name or f"r{self.nres}", semkey)

    def sb(self, name, shape, dtype):
        t = self.stack.enter_context(self.nc.sbuf_tensor(name, list(shape), dtype))
        return t

    def _need(self, eng, ev, waits):
        if ev is None:
            return
        sem, val = ev
        if sem == "E_" + eng:
            return
        if self.waited[eng].get(sem, 0) >= val:
            return
        self.waited[eng][sem] = val
        waits.append((sem, val))

    def op(self, eng, fn, reads=(), writes=(), dma=None):
        waits = []
        for r in reads:
            self._need(eng, r.last_w, waits)
        for w in writes:
            self._need(eng, w.last_w, waits)
            for ev in w.readers:
                self._need(eng, ev, waits)
        if dma is not None:
            sem = "D_" + dma.semkey
            prev = self.dma_last.get(sem)
            if prev is not None:
                if self.waited[eng].get(sem, 0) < prev:
                    self.waited[eng][sem] = prev
                    waits.append((sem, prev))
            val = self.dma_cnt.get(sem, 0) + 16
            self.dma_cnt[sem] = val
            self.dma_last[sem] = val
            ev = (sem, val)
            inc = (sem, 16)
        else:
            self.cnt[eng] += 1
            ev = ("E_" + eng, self.cnt[eng])
            inc = ("E_" + eng, 1)
        for r in reads:
            r.readers.append(ev)
        for w in writes:
            w.last_w = ev
            w.readers = []
        rec = _Rec()
        fn(rec)
        self.q[eng].append((waits, rec.call, inc))
        return ev

    def emit(self):
        nc = self.nc
        names = set()
        for e in self.ENG:
            for waits, fn, inc in self.q[e]:
                names.add(inc[0])
                for s, _ in waits:
                    names.add(s)
        for n in sorted(names):
            self.sems[n] = self.stack.enter_context(nc.semaphore(n))
        fin = [(s, v) for s, v in self.dma_last.items()]
        sems = self.sems
        q = self.q

        def replay(eng_name, e, extra=None):
            for waits, fn, inc in q[eng_name]:
                for s, v in waits:
                    e.wait_ge(sems[s], v)
                name, a, k = fn
                ins = getattr(e, name)(*a, **k)
                ins.then_inc(sems[inc[0]], inc[1])
            if extra:
                for s, v in extra:
                    e.wait_ge(sems[s], v)

        with nc.Block() as block:
            @block.tensor
            def _(e):
                replay("pe", e)

            @block.scalar
            def _(e):
                replay("act", e)

            @block.vector
            def _(e):
                replay("dve", e)

            @block.gpsimd
            def _(e):
                replay("pool", e)

            @block.sync
            def _(e):
                replay("sp", e, fin)
        self.stack.close()


def split3(x):
    x = np.float32(x)
    a = np.float32(x.astype(ml_dtypes.bfloat16).astype(np.float32))
    r = np.float32(x - a)
    b = np.float32(r.astype(ml_dtypes.bfloat16).astype(np.float32))
    r2 = np.float32(r - b)
    c = np.float32(r2.astype(ml_dtypes.bfloat16).astype(np.float32))
    return float(a), float(b), float(c)


def make_consts():
    c = np.zeros((128, 16), np.float32)
    for p in range(128):
        r = p % 16
        c[p, 0] = np.float32(10000.0) ** np.float32(-r / 16.0)
    for h in range(12):
        s = np.float32(HEAD_SLOPES[h])
        c[h, 1] = s
        c[h, 2], c[h, 3], c[h, 4] = split3(s)
    c[:, 5] = math.pi
    rm = np.zeros((128, 128), np.float32)
    for blk in range(4):
        for r in range(16):
            rm[blk * 32 + r + 16, blk * 32 + r] = -1.0
            rm[blk * 32 + r, blk * 32 + r + 16] = 1.0
    ident = np.eye(128, dtype=np.float32)
    jk = np.arange(128)[:, None]
    jq = np.arange(128)[None, :]
    m_prev = (jq <= jk).astype(np.float32)
    m_next = (jk <= jq).astype(np.float32)
    return c, rm, ident, m_prev, m_next


def build_aux(T):
    nc = bass.Bass("TRN2", target_bir_lowering=False)
    pos = nc.dram_tensor("pos", [1, T], I32, kind="ExternalInput").ap()
    cst = nc.dram_tensor("cst", [128, 16], F32, kind="ExternalInput").ap()
    kaug = nc.dram_tensor("kaug", [NAUG, T], BF16, kind="ExternalOutput").ap()
    kaugn = nc.dram_tensor("kaugn", [NAUG, T], BF16, kind="ExternalOutput").ap()
    qaug = nc.dram_tensor("qaug", [2, 12, NAUG, T], BF16, kind="ExternalOutput").ap()
    rcos = nc.dram_tensor("rcos", [128, T], F32, kind="ExternalOutput").ap()
    rsin = nc.dram_tensor("rsin", [128, T], F32, kind="ExternalOutput").ap()
    P = Prog(nc)
    posi = P.sb("posi", [128, T], I32)
    posf = P.sb("posf", [128, T], F32)
    ct = P.sb("ct", [128, 16], F32)
    tmpi = P.sb("tmpi", [16, T], I32)
    tmpf = P.sb("tmpf", [128, T], F32)
    tmpf2 = P.sb("tmpf2", [128, T], F32)
    rows = [P.sb(f"row{i}", [16, T], BF16) for i in range(6)]
    nrows = [P.sb(f"nrow{i}", [16, T], BF16) for i in range(6)]
    krow = [P.sb(f"krow{i}", [16, T], BF16) for i in range(3)]
    r_posi, r_posf, r_ct, r_tmpi, r_tmpf, r_tmpf2 = (P.res(n) for n in ("posi", "posf", "ct", "tmpi", "tmpf", "tmpf2"))
    r_rows = [P.res(f"row{i}") for i in range(6)]
    r_nrows = [P.res(f"nrow{i}") for i in range(6)]
    r_krow = [P.res(f"krow{i}") for i in range(3)]
    r_out = P.res("out")

    P.op("sp", lambda e: e.dma_start(out=posi[:, :], in_=pos.partition_broadcast(128)), writes=[r_posi], dma=r_posi)
    P.op("sp", lambda e: e.dma_start(out=ct[:, :], in_=cst[:, :]), writes=[r_ct], dma=r_ct)
    P.op("dve", lambda e: e.tensor_copy(out=posf[:, :], in_=posi[:, :]), reads=[r_posi], writes=[r_posf])
    P.op("pool", lambda e: e.memset(krow[0][:, :], 1.0), writes=[r_krow[0]])
    P.op("dve", lambda e: e.tensor_single_scalar(out=tmpi[:, :], in_=posi[0:16, :], scalar=7, op=ALU.arith_shift_right),
         reads=[r_posi], writes=[r_tmpi])
    P.op("dve", lambda e: e.tensor_copy(out=tmpf[0:16, :], in_=tmpi[:, :]), reads=[r_tmpi], writes=[r_tmpf])
    P.op("dve", lambda e: e.tensor_scalar(out=krow[1][:, :], in0=tmpf[0:16, :], scalar1=-128.0, scalar2=None, op0=ALU.mult),
         reads=[r_tmpf], writes=[r_krow[1]])
    P.op("dve", lambda e: e.tensor_single_scalar(out=tmpi[:, :], in_=posi[0:16, :], scalar=127, op=ALU.bitwise_and),
         reads=[r_posi], writes=[r_tmpi])
    P.op("dve", lambda e: e.tensor_copy(out=tmpf[0:16, :], in_=tmpi[:, :]), reads=[r_tmpi], writes=[r_tmpf])
    P.op("dve", lambda e: e.tensor_scalar(out=krow[2][:, :], in0=tmpf[0:16, :], scalar1=-1.0, scalar2=None, op0=ALU.mult),
         reads=[r_tmpf], writes=[r_krow[2]])
    for r in range(NAUG):
        src = krow[r // 3]
        P.op("sp", lambda e, src=src, r=r: e.dma_start(out=kaug[r:r + 1, :], in_=src[0:1, :]),
             reads=[r_krow[r // 3]], dma=r_krow[r // 3])
    nkrow = [P.sb(f"nkrow{i}", [16, T], BF16) for i in range(3)]
    r_nkrow = [P.res(f"nkrow{i}") for i in range(3)]
    for i in range(3):
        P.op("pool", lambda e, i=i: e.tensor_scalar(out=nkrow[i][:, :], in0=krow[i][:, :], scalar1=-1.0, scalar2=None, op0=ALU.mult),
             reads=[r_krow[i]], writes=[r_nkrow[i]])
    for r in range(NAUG):
        src = nkrow[r // 3]
        P.op("sp", lambda e, src=src, r=r: e.dma_start(out=kaugn[r:r + 1, :], in_=src[0:1, :]),
             reads=[r_nkrow[r // 3]], dma=r_nkrow[r // 3])
    P.op("dve", lambda e: e.tensor_scalar(out=tmpf[0:16, :], in0=posf[0:16, :], scalar1=ct[0:16, 1:2], scalar2=None, op0=ALU.mult),
         reads=[r_posf, r_ct], writes=[r_tmpf])
    P.op("dve", lambda e: e.tensor_copy(out=rows[0][:, :], in_=tmpf[0:16, :]), reads=[r_tmpf], writes=[r_rows[0]])
    P.op("dve", lambda e: e.tensor_tensor(out=tmpf2[0:16, :], in0=tmpf[0:16, :], in1=rows[0][:, :], op=ALU.subtract),
         reads=[r_tmpf, r_rows[0]], writes=[r_tmpf2])
    P.op("dve", lambda e: e.tensor_copy(out=rows[1][:, :], in_=tmpf2[0:16, :]), reads=[r_tmpf2], writes=[r_rows[1]])
    P.op("dve", lambda e: e.tensor_tensor(out=tmpf[0:16, :], in0=tmpf2[0:16, :], in1=rows[1][:, :], op=ALU.subtract),
         reads=[r_tmpf2, r_rows[1]], writes=[r_tmpf])
    P.op("dve", lambda e: e.tensor_copy(out=rows[2][:, :], in_=tmpf[0:16, :]), reads=[r_tmpf], writes=[r_rows[2]])
    for k in range(3):
        P.op("pool", lambda e, k=k: e.tensor_scalar(out=rows[3 + k][:, :], in0=krow[0][:, :], scalar1=ct[0:16, 2 + k:3 + k],
                                                   scalar2=None, op0=ALU.mult),
             reads=[r_krow[0], r_ct], writes=[r_rows[3 + k]])
    for k in range(6):
        P.op("pool", lambda e, k=k: e.tensor_scalar(out=nrows[k][:, :], in0=rows[k][:, :], scalar1=-1.0, scalar2=None, op0=ALU.mult),
             reads=[r_rows[k]], writes=[r_nrows[k]])
    for v, (rr, rres) in enumerate(((rows, r_rows), (nrows, r_nrows))):
        for r in range(NAUG):
            k = r if r < 3 else 3 + (r - 3) % 3
            P.op("sp", lambda e, v=v, r=r, k=k, rr=rr: e.dma_start(out=qaug[v, :, r, :], in_=rr[k][0:12, :]),
                 reads=[rres[k]], dma=rres[k])
    ang = P.sb("ang", [128, T], F32)
    r_ang = P.res("ang")
    P.op("dve", lambda e: e.tensor_scalar(out=ang[:, :], in0=posf[:, :], scalar1=ct[:, 0:1], scalar2=None, op0=ALU.mult),
         reads=[r_posf, r_ct], writes=[r_ang])
    kti = P.sb("kti", [128, T], I32)
    r_kti = P.res("kti")

    def sin_of(shift, dst, r_dst, out_ap):
        P.op("dve", lambda e: e.tensor_scalar(out=dst[:, :], in0=ang[:, :], scalar1=shift, scalar2=1.0 / TWO_PI, op0=ALU.add, op1=ALU.mult),
             reads=[r_ang], writes=[r_dst])
        P.op("dve", lambda e: e.tensor_copy(out=kti[:, :], in_=dst[:, :]), reads=[r_dst], writes=[r_kti])
        P.op("dve", lambda e: e.tensor_copy(out=dst[:, :], in_=kti[:, :]), reads=[r_kti], writes=[r_dst])
        P.op("dve", lambda e: e.scalar_tensor_tensor(out=dst[:, :], in0=dst[:, :], scalar=-TWO_PI, in1=ang[:, :], op0=ALU.mult, op1=ALU.add),
             reads=[r_dst, r_ang], writes=[r_dst])
        if shift != 0.0:
            P.op("dve", lambda e: e.tensor_scalar(out=dst[:, :], in0=dst[:, :], scalar1=shift, scalar2=None, op0=ALU.add),
                 reads=[r_dst], writes=[r_dst])
        P.op("dve", lambda e: e.tensor_single_scalar(out=posf[:, :], in_=dst[:, :], scalar=math.pi, op=ALU.is_gt),
             reads=[r_dst], writes=[r_posf2])
        P.op("dve", lambda e: e.scalar_tensor_tensor(out=dst[:, :], in0=posf[:, :], scalar=-TWO_PI, in1=dst[:, :], op0=ALU.mult, op1=ALU.add),
             reads=[r_dst, r_posf2], writes=[r_dst])
        P.op("dve", lambda e: e.tensor_single_scalar(out=posf[:, :], in_=dst[:, :], scalar=-math.pi, op=ALU.is_lt),
             reads=[r_dst], writes=[r_posf2])
        P.op("dve", lambda e: e.scalar_tensor_tensor(out=dst[:, :], in0=posf[:, :], scalar=TWO_PI, in1=dst[:, :], op0=ALU.mult, op1=ALU.add),
             reads=[r_dst, r_posf2], writes=[r_dst])
        P.op("dve", lambda e: e.tensor_scalar(out=dst[:, :], in0=dst[:, :], scalar1=math.pi, scalar2=-math.pi, op0=ALU.min, op1=ALU.max),
             reads=[r_dst], writes=[r_dst])
        P.op("act", lambda e: e.activation(out=dst[:, :], in_=dst[:, :], func=AF.Sin), reads=[r_dst], writes=[r_dst])
        P.op("sp", lambda e: e.dma_start(out=out_ap, in_=dst[:, :]), reads=[r_dst], dma=r_dst)

    r_posf2 = r_posf
    sin_of(0.0, tmpf, r_tmpf, rsin[:, :])
    sin_of(0.5 * math.pi, tmpf2, r_tmpf2, rcos[:, :])
    P.emit()
    return nc


class Ring:
    def __init__(self, P, name, n, shape, dtype):
        self.t = [P.sb(f"{name}{i}", shape, dtype) for i in range(n)]
        self.r = [P.res(f"{name}{i}") for i in range(n)]
        self.n = n
        self.i = 0

    def next(self):
        i = self.i
        self.i = (i + 1) % self.n
        return self.t[i], self.r[i]


class PsumRing:
    def __init__(self, P, banks):
        self.ps = P.stack.enter_context(P.nc.psum_tensor("ps", [128, 8, 512], F32))
        self.r = [P.res(f"psb{i}") for i in range(8)]
        self.banks = list(banks)
        self.i = 0

    def next(self):
        b = self.banks[self.i]
        self.i = (self.i + 1) % len(self.banks)
        return self.ps[:, b, :], self.r[b]


def load_cast(P, dst, r_dst, src_ap, stage_ring, shape_part, eng_cast="pool", dma_eng="sp"):
    st, r_st = stage_ring.next()
    n = shape_part
    P.op(dma_eng, lambda e: e.dma_start(out=st[:, 0:n], in_=src_ap), writes=[r_st], dma=r_st)
    P.op(eng_cast, lambda e: e.tensor_copy(out=dst, in_=st[:, 0:n]), reads=[r_st], writes=[r_dst])


def rms_rstd(P, PS, sq_list, r_sq, ones_bf, r_ones, inv_n, eps_ap, r_cst, rstd, r_rstd, np_=128):
    bank, r_bank = PS.next()
    n = len(sq_list)
    for k, sq in enumerate(sq_list):
        P.op("pe", lambda e, sq=sq, k=k: e.matmul(bank[0:np_, :], lhsT=ones_bf[:, 0:np_], rhs=sq, start=(k == 0), stop=(k == n - 1)),
             reads=[r_sq, r_ones], writes=[r_bank])
    P.op("act", lambda e: e.activation(out=rstd[0:np_, :], in_=bank[0:np_, :], func=AF.Sqrt, bias=eps_ap[0:np_, :], scale=inv_n),
         reads=[r_bank, r_cst], writes=[r_rstd])
    P.op("dve", lambda e: e.reciprocal(out=rstd[0:np_, :], in_=rstd[0:np_, :]), reads=[r_rstd], writes=[r_rstd])


def build_proj(T):
    TT = 512
    NT = T // TT
    nc = bass.Bass("TRN2", target_bir_lowering=False)
    din = lambda n, s, d=F32: nc.dram_tensor(n, s, d, kind="ExternalInput").ap()
    dout = lambda n, s, d=BF16: nc.dram_tensor(n, s, d, kind="ExternalOutput").ap()
    xT = din("xT", [D, T])
    w_in = din("w_in", [D, D_IN])
    w_qup = din("w_qup", [256, 384])
    w_kvup = din("w_kvup", [128, 512])
    gains = din("gains", [128, 16])
    rcos = din("rcos", [128, T])
    rsin = din("rsin", [128, T])
    rm_d = din("rm", [128, 128])
    qd = dout("qd", [256, T]); kd = dout("kd", [256, T]); vds = dout("vds", [T, 384])
    qm = dout("qm", [384, T]); kn = dout("kn", [256, T]); kr = dout("kr", [32, T]); vm = dout("vm", [T, 256])
    qs = dout("qs", [512, T]); ks = dout("ks", [128, T])

    P = Prog(nc)
    PS = PsumRing(P, range(8))
    wb = P.sb("wb", [128, 8, D_IN], BF16); r_wb = P.res("wb")
    wqb = P.sb("wqb", [128, 2, 384], BF16); r_wqb = P.res("wqb")
    wkvb = P.sb("wkvb", [128, 512], BF16); r_wkvb = P.res("wkvb")
    gt = P.sb("gt", [128, 16], F32); r_gt = P.res("gt")
    rm = P.sb("rmt", [128, 128], F32); r_rm = P.res("rmt")
    ones = P.sb("ones", [128, 128], BF16); r_ones = P.res("ones")
    stage = Ring(P, "wst", 2, [128, D_IN], F32)
    P.op("sp", lambda e: e.dma_start(out=gt[:, :], in_=gains[:, :]), writes=[r_gt], dma=r_gt)
    P.op("sp", lambda e: e.dma_start(out=rm[:, :], in_=rm_d[:, :]), writes=[r_rm], dma=r_rm)
    P.op("pool", lambda e: e.memset(ones[:, :], 1.0), writes=[r_ones])
    for kc in range(8):
        load_cast(P, wb[:, kc, :], r_wb, w_in[kc * 128:(kc + 1) * 128, :], stage, D_IN)
    for kc in range(2):
        load_cast(P, wqb[:, kc, :], r_wqb, w_qup[kc * 128:(kc + 1) * 128, :], stage, 384)
    load_cast(P, wkvb[:, :], r_wkvb, w_kvup[:, :], stage, 512)

    xr = Ring(P, "xt", 2, [128, 8, TT], F32)
    sq = P.sb("sq", [128, 8, TT], BF16); r_sq = P.res("sq")
    hT = P.sb("hT", [128, 8, TT], BF16); r_hT = P.res("hT")
    rstd = P.sb("rstd", [128, TT], F32); r_rstd = P.res("rstd")
    ost = Ring(P, "ost", 4, [128, TT], BF16)
    cqT = P.sb("cqT", [128, 2, TT], F32); r_cqT = P.res("cqT")
    cqn = P.sb("cqn", [128, 2, TT], BF16); r_cqn = P.res("cqn")
    ckvT = P.sb("ckvT", [128, TT], F32); r_ckvT = P.res("ckvT")
    ckvn = P.sb("ckvn", [128, TT], BF16); r_ckvn = P.res("ckvn")
    krT = P.sb("krT", [32, TT], F32); r_krT = P.res("krT")
    xrope = P.sb("xrope", [128, TT], F32); r_xrope = P.res("xrope")
    cs = Ring(P, "cs", 2, [128, 2, TT], F32)
    t1 = P.sb("t1", [128, TT], F32); r_t1 = P.res("t1")
    t2 = P.sb("t2", [128, TT], F32); r_t2 = P.res("t2")
    eps_ap = gt[:, 11:12]
    xv = xT.rearrange("(k p) t -> p k t", p=128)

    def evac(bank, r_bank, np_, scale, dst_dram):
        o, r_o = ost.next()
        P.op("act", lambda e: e.activation(out=o[0:np_, :], in_=bank[0:np_, :], func=AF.Copy, scale=scale),
             reads=[r_bank], writes=[r_o])
        P.op("sp", lambda e: e.dma_start(out=dst_dram, in_=o[0:np_, :]), reads=[r_o], dma=r_o)

    def rope_apply(src, r_src, np_, cst, r_cs, dsts):
        bank, r_bank = PS.next()
        P.op("pe", lambda e: e.matmul(bank[0:np_, :], lhsT=rm[0:np_, 0:np_], rhs=src[0:np_, :], start=True, stop=True),
             reads=[r_src, r_rm], writes=[r_bank])
        P.op("dve", lambda e: e.tensor_tensor(out=t1[0:np_, :], in0=src[0:np_, :], in1=cst[0:np_, 0, :], op=ALU.mult),
             reads=[r_src, r_cs], writes=[r_t1])
        P.op("dve", lambda e: e.tensor_tensor(out=t2[0:np_, :], in0=bank[0:np_, :], in1=cst[0:np_, 1, :], op=ALU.mult),
             reads=[r_bank, r_cs], writes=[r_t2])
        o, r_o = ost.next()
        P.op("dve", lambda e: e.tensor_tensor(out=o[0:np_, :], in0=t1[0:np_, :], in1=t2[0:np_, :], op=ALU.add),
             reads=[r_t1, r_t2], writes=[r_o])
        for (p0, p1, dst) in dsts:
            P.op("sp", lambda e, p0=p0, p1=p1, dst=dst: e.dma_start(out=dst, in_=o[p0:p1, :]), reads=[r_o], dma=r_o)

    for it in range(NT):
        tok = slice(it * TT, (it + 1) * TT)
        xt, r_xt = xr.next()
        P.op("sp", lambda e, xt=xt, tok=tok: e.dma_start(out=xt[:, :, :], in_=xv[:, :, tok]), writes=[r_xt], dma=r_xt)
        cst, r_cs = cs.next()
        P.op("sp", lambda e, cst=cst, tok=tok: e.dma_start(out=cst[:, 0, :], in_=rcos[:, tok]), writes=[r_cs], dma=r_cs)
        P.op("sp", lambda e, cst=cst, tok=tok: e.dma_start(out=cst[:, 1, :], in_=rsin[:, tok]), writes=[r_cs], dma=r_cs)
        for half in range(2):
            P.op("act", lambda e, xt=xt, half=half: e.activation(out=sq[:, 4 * half:4 * half + 4, :], in_=xt[:, 4 * half:4 * half + 4, :], func=AF.Square),
                 reads=[r_xt], writes=[r_sq])
        rms_rstd(P, PS, [sq[:, k, :] for k in range(8)], r_sq, ones, r_ones, 1.0 / D, eps_ap, r_gt, rstd, r_rstd)
        for k in range(8):
            eng = "dve"
            P.op(eng, lambda e, xt=xt, k=k: e.scalar_tensor_tensor(out=hT[:, k, :], in0=xt[:, k, :], scalar=gt[:, k:k + 1], in1=rstd[:, :],
                                                                   op0=ALU.mult, op1=ALU.mult),
                 reads=[r_xt, r_gt, r_rstd], writes=[r_hT])

        def proj(col, M, np0=0):
            bank, r_bank = PS.next()
            for k in range(8):
                P.op("pe", lambda e, k=k: e.matmul(bank[np0:np0 + M, :], lhsT=wb[:, k, col:col + M], rhs=hT[:, k, :], start=(k == 0), stop=(k == 7)),
                     reads=[r_wb, r_hT], writes=[r_bank])
            return bank, r_bank

        for c in range(2):
            bank, r_bank = proj(O_AQ + c * 128, 128)
            evac(bank, r_bank, 128, 1.0 / math.sqrt(32.0), qd[c * 128:(c + 1) * 128, tok])
        for c in range(2):
            bank, r_bank = proj(O_AK + c * 128, 128)
            evac(bank, r_bank, 128, 1.0, kd[c * 128:(c + 1) * 128, tok])
        for c in range(4):
            bank, r_bank = proj(O_SQ + c * 128, 128)
            evac(bank, r_bank, 128, 1.0 / 8.0, qs[c * 128:(c + 1) * 128, tok])
        bank, r_bank = proj(O_SK, 128)
        evac(bank, r_bank, 128, 1.0, ks[:, tok])
        for c in range(2):
            bank, r_bank = proj(O_CQ + c * 128, 128)
            P.op("dve", lambda e, bank=bank, c=c: e.tensor_copy(out=cqT[:, c, :], in_=bank), reads=[r_bank], writes=[r_cqT])
        bank, r_bank = proj(O_CKV, 128)
        P.op("dve", lambda e, bank=bank: e.tensor_copy(out=ckvT[:, :], in_=bank), reads=[r_bank], writes=[r_ckvT])
        bank, r_bank = proj(O_KR, 32)
        P.op("dve", lambda e, bank=bank: e.tensor_copy(out=krT[:, :], in_=bank[0:32, :]), reads=[r_bank], writes=[r_krT])
        for j in range(4):
            bank, r_bank = PS.next()
            for k in range(8):
                P.op("pe", lambda e, k=k, j=j, bank=bank: e.matmul(bank[:, 0:256], lhsT=hT[:, k, j * 128:(j + 1) * 128], rhs=wb[:, k, O_AV:O_AV + 256],
                                                                   start=(k == 0), stop=(k == 7)), reads=[r_wb, r_hT], writes=[r_bank])
            for k in range(8):
                P.op("pe", lambda e, k=k, j=j, bank=bank: e.matmul(bank[:, 256:384], lhsT=hT[:, k, j * 128:(j + 1) * 128], rhs=wb[:, k, O_SV:O_SV + 128],
                                                                   start=(k == 0), stop=(k == 7)), reads=[r_wb, r_hT], writes=[r_bank])
            o, r_o = ost.next()
            P.op("act", lambda e, o=o, bank=bank: e.activation(out=o[:, 0:384], in_=bank[:, 0:384], func=AF.Copy), reads=[r_bank], writes=[r_o])
            P.op("sp", lambda e, o=o, j=j, it=it: e.dma_start(out=vds[it * TT + j * 128:it * TT + (j + 1) * 128, :], in_=o[:, 0:384]), reads=[r_o], dma=r_o)
        P.op("act", lambda e: e.activation(out=sq[:, 0:2, :], in_=cqT[:, :, :], func=AF.Square), reads=[r_cqT], writes=[r_sq])
        rms_rstd(P, PS, [sq[:, k, :] for k in range(2)], r_sq, ones, r_ones, 1.0 / 256.0, eps_ap, r_gt, rstd, r_rstd)
        for k in range(2):
            P.op("dve", lambda e, k=k: e.scalar_tensor_tensor(out=cqn[:, k, :], in0=cqT[:, k, :], scalar=gt[:, 8 + k:9 + k], in1=rstd[:, :],
                                                              op0=ALU.mult, op1=ALU.mult), reads=[r_cqT, r_gt, r_rstd], writes=[r_cqn])
        qscale = 1.0 / math.sqrt(96.0)
        for h in range(4):
            bank, r_bank = PS.next()
            for k in range(2):
                P.op("pe", lambda e, k=k, h=h, bank=bank: e.matmul(bank[0:64, :], lhsT=wqb[:, k, h * 96:h * 96 + 64], rhs=cqn[:, k, :], start=(k == 0), stop=(k == 1)),
                     reads=[r_wqb, r_cqn], writes=[r_bank])
            evac(bank, r_bank, 64, qscale, qm[h * 96:h * 96 + 64, tok])
        for hp in range(2):
            bank, r_bank = PS.next()
            for hh in range(2):
                h = 2 * hp + hh
                for k in range(2):
                    P.op("pe", lambda e, k=k, h=h, hh=hh, bank=bank: e.matmul(bank[32 * hh:32 * hh + 32, :], lhsT=wqb[:, k, h * 96 + 64:h * 96 + 96], rhs=cqn[:, k, :],
                                                                              start=(k == 0), stop=(k == 1)), reads=[r_wqb, r_cqn], writes=[r_bank])
            P.op("act", lambda e, bank=bank, hp=hp: e.activation(out=xrope[64 * hp:64 * hp + 64, :], in_=bank[0:64, :], func=AF.Copy, scale=qscale),
                 reads=[r_bank], writes=[r_xrope])
        rope_apply(xrope, r_xrope, 128, cst, r_cs, [(32 * h, 32 * h + 32, qm[h * 96 + 64:h * 96 + 96, tok]) for h in range(4)])
        P.op("act", lambda e: e.activation(out=sq[:, 0, :], in_=ckvT[:, :], func=AF.Square), reads=[r_ckvT], writes=[r_sq])
        rms_rstd(P, PS, [sq[:, 0, :]], r_sq, ones, r_ones, 1.0 / 128.0, eps_ap, r_gt, rstd, r_rstd)
        P.op("dve", lambda e: e.scalar_tensor_tensor(out=ckvn[:, :], in0=ckvT[:, :], scalar=gt[:, 10:11], in1=rstd[:, :], op0=ALU.mult, op1=ALU.mult),
             reads=[r_ckvT, r_gt, r_rstd], writes=[r_ckvn])
        for hp in range(2):
            bank, r_bank = PS.next()
            for hh in range(2):
                h = hp * 2 + hh
                P.op("pe", lambda e, h=h, hh=hh, bank=bank: e.matmul(bank[64 * hh:64 * hh + 64, :], lhsT=wkvb[:, h * 128:h * 128 + 64], rhs=ckvn[:, :], start=True, stop=True),
                     reads=[r_wkvb, r_ckvn], writes=[r_bank])
            evac(bank, r_bank, 128, 1.0, kn[hp * 128:(hp + 1) * 128, tok])
        for j in range(4):
            bank, r_bank = PS.next()
            for h in range(4):
                P.op("pe", lambda e, h=h, j=j, bank=bank: e.matmul(bank[:, h * 64:h * 64 + 64], lhsT=ckvn[:, j * 128:(j + 1) * 128], rhs=wkvb[:, h * 128 + 64:h * 128 + 128],
                                                                   start=True, stop=True), reads=[r_wkvb, r_ckvn], writes=[r_bank])
            o, r_o = ost.next()
            P.op("act", lambda e, o=o, bank=bank: e.activation(out=o[:, 0:256], in_=bank[:, 0:256], func=AF.Copy), reads=[r_bank], writes=[r_o])
            P.op("sp", lambda e, o=o, j=j, it=it: e.dma_start(out=vm[it * TT + j * 128:it * TT + (j + 1) * 128, :], in_=o[:, 0:256]), reads=[r_o], dma=r_o)
        rope_apply(krT, r_krT, 32, cst, r_cs, [(0, 32, kr[:, tok])])
    P.emit()
    return nc


def build_att(T, do_diff=True, do_mla=True, do_swa=True):
    S = 4 * T
    NB = S // 128
    NBO = T // 128
    NC = T // 512
    TH = T + 256
    nc = bass.Bass("TRN2", target_bir_lowering=False)
    din = lambda n, s, d=BF16: nc.dram_tensor(n, s, d, kind="ExternalInput").ap()
    qd = din("qd", [256, T]) if do_diff else None
    qm = din("qm", [384, T]) if do_mla else None
    qs = din("qs", [512, T]) if do_swa else None
    qaug = din("qaug", [2, 12, NAUG, T]) if (do_diff or do_swa) else None
    kd = din("kd", [256, S]) if do_diff else None
    kaug = din("kaug", [NAUG, S]) if do_diff else None
    vds = din("vds", [S, 256]) if do_diff else None
    kn = din("kn", [256, S]) if do_mla else None
    kr = din("kr", [32, S]) if do_mla else None
    vm = din("vm", [S, 256]) if do_mla else None
    ksl = din("ksl", [128, TH]) if do_swa else None
    kaugl = din("kaugl", [NAUG, TH]) if do_swa else None
    vsl = din("vsl", [TH, 128]) if do_swa else None
    sinks = din("sinks", [128, 8], F32)
    dlam = din("dlam", [1, 128], F32)
    dhg = din("dhg", [64, 4], F32)
    masks = din("masks", [4, 128, 512], F32)
    assert do_diff + do_mla + do_swa == 1
    orows = 512 if do_swa else 256
    obase = 0 if do_diff else (256 if do_mla else 512)
    oT_full = nc.dram_tensor("oT", [orows, T], BF16, kind="ExternalOutput").ap()

    class _Shift:
        def __getitem__(self, key):
            r, c = key
            return oT_full[r.start - obase:r.stop - obase, c]
    oT = _Shift()

    P = Prog(nc)
    PS = PsumRing(P, range(8))
    ps = PS.ps
    rb = PS.r
    kring = Ring(P, "kt", 2, [128, S], BF16)
    vring = Ring(P, "vt", 2, [128, NB, 128], BF16)
    for i in range(2):
        P.op("pool", lambda e, i=i: e.memset(vring.t[i][:, :, 64:128], 1.0), writes=[vring.r[i]])
    qring = Ring(P, "qt", 2, [128, 2, 512], BF16)
    padded = do_diff and not do_mla and not do_swa
    if padded:
        for i in range(2):
            for (p0, p1) in ((32, 64), (64, 128)):
                P.op("pool", lambda e: e.memset(kring.t[i][p0:p1, :], 0.0), writes=[kring.r[i]])
                P.op("pool", lambda e: e.memset(qring.t[i][p0:p1, :, :], 0.0), writes=[qring.r[i]])
    pring = Ring(P, "pt", 3, [128, 1024], BF16)
    ones = P.sb("ones", [128, 128], BF16); r_ones = P.res("ones")
    P.op("pool", lambda e: e.memset(ones[:, :], 1.0), writes=[r_ones])
    o0buf = P.sb("o0buf", [64, T], F32); r_o0 = P.res("o0buf")
    tA = P.sb("tA", [128, 512], F32); r_tA = P.res("tA")
    tS = P.sb("tS", [128, 512], F32); r_tS = P.res("tS")
    rec = P.sb("rec", [128, 512], F32); r_rec = P.res("rec")
    of = P.sb("of", [64, 512], F32); r_of = P.res("of")
    dd = P.sb("dd", [64, 512], F32); r_dd = P.res("dd")
    sqd = P.sb("sqd", [64, 512], BF16); r_sqd = P.res("sqd")
    rstd = P.sb("rstd", [64, 512], F32); r_rstd = P.res("rstd")
    obr = Ring(P, "ob", 2, [64, 4, 512], BF16)
    sm = P.sb("sm", [128, 8], F32); r_sm = P.res("sm")
    lamt = P.sb("lamt", [64, 128], F32); r_lamt = P.res("lamt")
    lam2 = P.sb("lam2", [64, 4], F32); r_lam2 = P.res("lam2")
    gt = P.sb("gt", [64, 4], F32); r_gt = P.res("gt")
    mstage = P.sb("mstage", [128, 512], F32); r_mstage = P.res("mstage")
    mk = [P.sb(f"mk{i}", [128, 512], BF16) for i in range(4)]
    r_mk = [P.res(f"mk{i}") for i in range(4)]

    P.op("sp", lambda e: e.dma_start(out=sm[:, :], in_=sinks[:, :]), writes=[r_sm], dma=r_sm)
    P.op("act", lambda e: e.activation(out=sm[:, :], in_=sm[:, :], func=AF.Exp), reads=[r_sm], writes=[r_sm])
    P.op("sp", lambda e: e.dma_start(out=lamt[:, :], in_=dlam.partition_broadcast(64)), writes=[r_lamt], dma=r_lamt)
    P.op("sp", lambda e: e.dma_start(out=gt[:, :], in_=dhg[:, :]), writes=[r_gt], dma=r_gt)
    P.op("dve", lambda e: e.tensor_tensor(out=lamt[:, 0:32], in0=lamt[:, 0:32], in1=lamt[:, 32:64], op=ALU.mult), reads=[r_lamt], writes=[r_lamt])
    P.op("dve", lambda e: e.tensor_tensor(out=lamt[:, 64:96], in0=lamt[:, 64:96], in1=lamt[:, 96:128], op=ALU.mult), reads=[r_lamt], writes=[r_lamt])
    P.op("dve", lambda e: e.reduce_sum(out=lam2[:, 0:1], in_=lamt[:, 0:32], axis=mybir.AxisListType.X), reads=[r_lamt], writes=[r_lam2])
    P.op("dve", lambda e: e.reduce_sum(out=lam2[:, 1:2], in_=lamt[:, 64:96], axis=mybir.AxisListType.X), reads=[r_lamt], writes=[r_lam2])
    P.op("act", lambda e: e.activation(out=lam2[:, 0:2], in_=lam2[:, 0:2], func=AF.Exp), reads=[r_lam2], writes=[r_lam2])
    P.op("dve", lambda e: e.tensor_tensor(out=lam2[:, 2:3], in0=lam2[:, 1:2], in1=lam2[:, 0:1], op=ALU.subtract), reads=[r_lam2], writes=[r_lam2])
    P.op("dve", lambda e: e.tensor_scalar(out=lam2[:, 2:3], in0=lam2[:, 2:3], scalar1=gt[:, 2:3], scalar2=None, op0=ALU.add), reads=[r_lam2, r_gt], writes=[r_lam2])
    P.op("dve", lambda e: e.tensor_scalar(out=gt[:, 0:1], in0=gt[:, 0:1], scalar1=gt[:, 3:4], scalar2=None, op0=ALU.mult), reads=[r_gt], writes=[r_gt])
    for i in range(4):
        P.op("sp", lambda e, i=i: e.dma_start(out=mstage[:, :], in_=masks[i, :, :]), writes=[r_mstage], dma=r_mstage)
        P.op("dve", lambda e, i=i: e.tensor_copy(out=mk[i][:, :], in_=mstage[:, :]), reads=[r_mstage], writes=[r_mk[i]])

    SB = [0, 1, 2, 3]
    ACC = [4, 5]
    MISC = [6, 7]

    def finish_common(acc, r_acc):
        P.op("dve", lambda e: e.reciprocal(out=rec[0:64, :], in_=acc[64:128, :]), reads=[r_acc], writes=[r_rec])
        P.op("dve", lambda e: e.tensor_tensor(out=of[:, :], in0=acc[0:64, :], in1=rec[0:64, :], op=ALU.mult), reads=[r_acc, r_rec], writes=[r_of])

    def dense_pass(krows, load_k, load_v, load_q, use_aug, finish):
        kt, r_kt = kring.next()
        vt, r_vt = vring.next()
        load_k(kt, r_kt)
        load_v(vt, r_vt)

        kmm = 128 if padded else krows

        def compute():
            sbi = [0]
            qts = {0: qring.next()}
            load_q(0, *qts[0])
            for c in range(NC):
                qt, r_qt = qts.pop(c)
                if c + 1 < NC:
                    qts[c + 1] = qring.next()
                    load_q(c + 1, *qts[c + 1])
                acc_b = ACC[c % 2]
                acc, r_acc = ps[:, acc_b, :], rb[acc_b]
                special = list(range(4 * c, 4 * c + 4)) if use_aug else []
                normal = [b for b in range(NB) if b not in special]
                items = [("n", normal[i:i + 2]) for i in range(0, len(normal), 2)] + [("s", [b]) for b in special]
                npv = [0]
                total = NB

                def pv(pt_ap, r_pt, b):
                    first = npv[0] == 0
                    last = npv[0] == total - 1
                    npv[0] += 1
                    P.op("pe", lambda e: e.matmul(acc, lhsT=vt[:, b, :], rhs=pt_ap, start=first, stop=last),
                         reads=[r_vt, r_pt], writes=[r_acc])

                def stage_ab(kind, blks):
                    pt, r_pt = pring.next()
                    if kind == "n":
                        b0 = SB[sbi[0] % 2 * 2]
                        sbi[0] += 1
                        for j, b in enumerate(blks):
                            ver = 1 if (use_aug and b < 4 * c) else 0
                            P.op("pe", lambda e: e.matmul(ps[:, b0 + j, :], lhsT=kt[0:kmm, b * 128:(b + 1) * 128], rhs=qt[0:kmm, ver, :],
                                                          start=True, stop=True), reads=[r_kt, r_qt], writes=[rb[b0 + j]])
                        n = len(blks)
                        P.op("act", lambda e: e.activation(out=pt[:, 0:512 * n].rearrange("p (a b) -> p a b", a=n), in_=ps[:, b0:b0 + n, :], func=AF.Exp),
                             reads=[rb[b0 + j] for j in range(n)], writes=[r_pt])
                    else:
                        b = blks[0]
                        for ver in range(2):
                            P.op("pe", lambda e: e.matmul(ps[:, MISC[ver], :], lhsT=kt[0:kmm, b * 128:(b + 1) * 128], rhs=qt[0:kmm, ver, :],
                                                          start=True, stop=True), reads=[r_kt, r_qt], writes=[rb[MISC[ver]]])
                        P.op("act", lambda e: e.activation(out=tA[:, :], in_=ps[:, MISC[0], :], func=AF.Copy), reads=[rb[MISC[0]]], writes=[r_tA])
                        P.op("dve", lambda e: e.tensor_tensor(out=tS[:, :], in0=tA[:, :], in1=ps[:, MISC[1], :], op=ALU.min),
                             reads=[r_tA, rb[MISC[1]]], writes=[r_tS])
                        P.op("act", lambda e: e.activation(out=pt[:, 0:512], in_=tS[:, :], func=AF.Exp), reads=[r_tS], writes=[r_pt])
                    return pt, r_pt

                def stage_c(blks, pt, r_pt):
                    for j, b in enumerate(blks):
                        pv(pt[:, 512 * j:512 * (j + 1)], r_pt, b)

                prev = None
                for kind, blks in items:
                    pt, r_pt = stage_ab(kind, blks)
                    if prev is not None:
                        stage_c(*prev)
                    prev = (blks, pt, r_pt)
                stage_c(*prev)
                finish(c, acc, r_acc)
        return compute

    kv_blk = lambda ap: ap.rearrange("(b p) c -> p b c", p=128)
    passes = []
    for h in range(4 if do_diff else 0):
        for m in range(2):
            def load_k(kt, r_kt, h=h, m=m):
                P.op("sp", lambda e: e.dma_start(out=kt[0:32, :], in_=kd[h * 64 + m * 32:h * 64 + m * 32 + 32, :]), writes=[r_kt], dma=r_kt)
                P.op("sp", lambda e: e.dma_start(out=kt[32:41, :], in_=kaug[:, :]), writes=[r_kt], dma=r_kt)

            def load_v(vt, r_vt, h=h):
                for q4 in range(4):
                    nb = NB // 4
                    P.op("sp", lambda e, q4=q4: e.dma_start(out=vt[:, q4 * nb:(q4 + 1) * nb, 0:64], in_=kv_blk(vds)[:, q4 * nb:(q4 + 1) * nb, h * 64:(h + 1) * 64]),
                         writes=[r_vt], dma=r_vt)

            def load_q(c, qt, r_qt, h=h, m=m):
                tok = slice(c * 512, (c + 1) * 512)
                for ver in range(2):
                    P.op("sp", lambda e, ver=ver: e.dma_start(out=qt[0:32, ver, :], in_=qd[h * 64 + m * 32:h * 64 + m * 32 + 32, tok]), writes=[r_qt], dma=r_qt)
                    P.op("sp", lambda e, ver=ver: e.dma_start(out=qt[32:41, ver, :], in_=qaug[ver, h, :, tok]), writes=[r_qt], dma=r_qt)

            def finish(c, acc, r_acc, h=h, m=m):
                tok = slice(c * 512, (c + 1) * 512)
                finish_common(acc, r_acc)
                if m == 0:
                    P.op("pool", lambda e: e.tensor_copy(out=o0buf[:, tok], in_=of[:, :]), reads=[r_of], writes=[r_o0])
                    return
                P.op("dve", lambda e: e.scalar_tensor_tensor(out=dd[:, :], in0=of[:, :], scalar=lam2[:, 2:3], in1=o0buf[:, tok], op0=ALU.mult, op1=ALU.add),
                     reads=[r_of, r_lam2, r_o0], writes=[r_dd])
                P.op("act", lambda e: e.activation(out=sqd[:, :], in_=dd[:, :], func=AF.Square), reads=[r_dd], writes=[r_sqd])
                mb = MISC[0]
                P.op("pe", lambda e: e.matmul(ps[0:64, mb, :], lhsT=ones[0:64, 0:64], rhs=sqd[:, :], start=True, stop=True), reads=[r_sqd, r_ones], writes=[rb[mb]])
                P.op("act", lambda e: e.activation(out=rstd[:, :], in_=ps[0:64, mb, :], func=AF.Sqrt, bias=gt[:, 1:2], scale=1.0 / 64.0),
                     reads=[rb[mb], r_gt], writes=[r_rstd])
                P.op("dve", lambda e: e.reciprocal(out=rstd[:, :], in_=rstd[:, :]), reads=[r_rstd], writes=[r_rstd])
                ob, r_ob = obr.next()
                P.op("dve", lambda e: e.scalar_tensor_tensor(out=ob[:, 0, :], in0=dd[:, :], scalar=gt[:, 0:1], in1=rstd[:, :], op0=ALU.mult, op1=ALU.mult),
                     reads=[r_dd, r_gt, r_rstd], writes=[r_ob])
                P.op("sp", lambda e: e.dma_start(out=oT[h * 64:(h + 1) * 64, tok], in_=ob[:, 0, :]), reads=[r_ob], dma=r_ob)
            passes.append((41, load_k, load_v, load_q, True, finish))
    for h in range(4 if do_mla else 0):
        def load_k(kt, r_kt, h=h):
            P.op("sp", lambda e: e.dma_start(out=kt[0:64, :], in_=kn[h * 64:(h + 1) * 64, :]), writes=[r_kt], dma=r_kt)
            P.op("sp", lambda e: e.dma_start(out=kt[64:96, :], in_=kr[:, :]), writes=[r_kt], dma=r_kt)

        def load_v(vt, r_vt, h=h):
            for q4 in range(4):
                nb = NB // 4
                P.op("sp", lambda e, q4=q4: e.dma_start(out=vt[:, q4 * nb:(q4 + 1) * nb, 0:64], in_=kv_blk(vm)[:, q4 * nb:(q4 + 1) * nb, h * 64:(h + 1) * 64]),
                     writes=[r_vt], dma=r_vt)

        def load_q(c, qt, r_qt, h=h):
            tok = slice(c * 512, (c + 1) * 512)
            P.op("sp", lambda e: e.dma_start(out=qt[0:96, 0, :], in_=qm[h * 96:(h + 1) * 96, tok]), writes=[r_qt], dma=r_qt)

        def finish(c, acc, r_acc, h=h):
            tok = slice(c * 512, (c + 1) * 512)
            finish_common(acc, r_acc)
            ob, r_ob = obr.next()
            P.op("pool", lambda e: e.tensor_copy(out=ob[:, 0, :], in_=of[:, :]), reads=[r_of], writes=[r_ob])
            P.op("sp", lambda e: e.dma_start(out=oT[256 + h * 64:256 + (h + 1) * 64, tok], in_=ob[:, 0, :]), reads=[r_ob], dma=r_ob)
        passes.append((96, load_k, load_v, load_q, False, finish))
    pending = None
    for args in passes:
        comp = dense_pass(*args)
        if pending is not None:
            pending()
        pending = comp
    if pending is not None:
        pending()

    NQB = T // 128
    ksw = P.sb("ksw", [128, TH], BF16); r_ksw = P.res("ksw")
    vsw = P.sb("vsw", [128, TH // 128, 128], BF16); r_vsw = P.res("vsw")
    P.op("pool", lambda e: e.memset(vsw[:, :, 64:128], 1.0), writes=[r_vsw])
    dsw = P.sb("dsw", [128, 512], F32); r_dsw = P.res("dsw")
    for g in range(2 if do_swa else 0):
        qp, r_qp = kring.next()
        qn, r_qn = kring.next()
        qv = [qp, qn]
        r_qv = [r_qp, r_qn]
        P.op("sp", lambda e, g=g: e.dma_start(out=ksw[0:64, :], in_=ksl[g * 64:(g + 1) * 64, :]), writes=[r_ksw], dma=r_ksw)
        P.op("sp", lambda e: e.dma_start(out=ksw[64:73, :], in_=kaugl[:, :]), writes=[r_ksw], dma=r_ksw)
        P.op("sp", lambda e, g=g: e.dma_start(out=vsw[:, :, 0:64], in_=kv_blk(vsl)[:, :, g * 64:(g + 1) * 64]), writes=[r_vsw], dma=r_vsw)
        for ver in range(2):
            for j in range(4):
                hq = g * 4 + j
                P.op("sp", lambda e, ver=ver, j=j, hq=hq: e.dma_start(out=qv[ver][0:64, j * T:(j + 1) * T], in_=qs[hq * 64:(hq + 1) * 64, :]),
                     writes=[r_qv[ver]], dma=r_qv[ver])
                P.op("sp", lambda e, ver=ver, j=j, hq=hq: e.dma_start(out=qv[ver][64:73, j * T:(j + 1) * T], in_=qaug[ver, 4 + hq, :, :]),
                     writes=[r_qv[ver]], dma=r_qv[ver])
        pairs = [(0, 1), (2, 3), (6, 7)]
        pi = [0]
        obs = {}

        def swa_ab(i, kk):
            kb = i + kk
            pa = pairs[pi[0] % 3]
            pi[0] += 1
            for ver in range(2):
                for j in range(4):
                    P.op("pe", lambda e: e.matmul(ps[:, pa[ver], j * 128:(j + 1) * 128], lhsT=ksw[0:73, kb * 128:(kb + 1) * 128],
                                                  rhs=qv[ver][0:73, j * T + i * 128:j * T + (i + 1) * 128], start=True, stop=True),
                         reads=[r_ksw, r_qv[ver]], writes=[rb[pa[ver]]])
            P.op("act", lambda e: e.activation(out=tA[:, :], in_=ps[:, pa[0], :], func=AF.Copy), reads=[rb[pa[0]]], writes=[r_tA])
            P.op("dve", lambda e: e.tensor_tensor(out=tS[:, :], in0=tA[:, :], in1=ps[:, pa[1], :], op=ALU.min),
                 reads=[r_tA, rb[pa[1]]], writes=[r_tS])
            pt, r_pt = pring.next()
            P.op("act", lambda e: e.activation(out=pt[:, 0:512], in_=tS[:, :], func=AF.Exp), reads=[r_tS], writes=[r_pt])
            mi = None
            if kk == 0:
                mi = 2 if i == 0 else 0
            elif kk == 2:
                mi = 3 if i == NQB - 1 else 1
            if mi is not None:
                P.op("pool", lambda e: e.tensor_tensor(out=pt[:, 0:512], in0=pt[:, 0:512], in1=mk[mi][:, :], op=ALU.mult),
                     reads=[r_pt, r_mk[mi]], writes=[r_pt])
            return (i, kk, pt, r_pt)

        def swa_c(i, kk, pt, r_pt):
            kb = i + kk
            acc_b = ACC[i % 2]
            acc, r_acc = ps[:, acc_b, :], rb[acc_b]
            P.op("pe", lambda e: e.matmul(acc, lhsT=vsw[:, kb, :], rhs=pt[:, 0:512], start=(kk == 0), stop=(kk == 2)),
                 reads=[r_vsw, r_pt], writes=[r_acc])
            if kk != 2:
                return
            if i % 4 == 0:
                obs[i // 4] = obr.next()
            ob, r_ob = obs[i // 4]
            for j in range(4):
                hq = g * 4 + j
                P.op("dve", lambda e: e.tensor_scalar(out=dsw[64:128, j * 128:(j + 1) * 128], in0=acc[64:128, j * 128:(j + 1) * 128],
                                                      scalar1=sm[64:128, hq:hq + 1], scalar2=None, op0=ALU.add),
                     reads=[r_acc, r_sm], writes=[r_dsw])
            P.op("dve", lambda e: e.reciprocal(out=rec[0:64, :], in_=dsw[64:128, :]), reads=[r_dsw], writes=[r_rec])
            P.op("dve", lambda e: e.tensor_tensor(out=ob[:, :, (i % 4) * 128:(i % 4 + 1) * 128], in0=acc[0:64, :].rearrange("p (j q) -> p j q", j=4),
                                                  in1=rec[0:64, :].rearrange("p (j q) -> p j q", j=4), op=ALU.mult),
                 reads=[r_acc, r_rec], writes=[r_ob])
            if i % 4 == 3:
                c = i // 4
                for j in range(4):
                    hq = g * 4 + j
                    P.op("sp", lambda e: e.dma_start(out=oT[512 + hq * 64:512 + (hq + 1) * 64, c * 512:(c + 1) * 512], in_=ob[:, j, :]),
                         reads=[r_ob], dma=r_ob)

        prev = None
        for i in range(NQB):
            for kk in range(3):
                cur = swa_ab(i, kk)
                if prev is not None:
                    swa_c(*prev)
                prev = cur
        swa_c(*prev)
    P.emit()
    return nc


def _prenorm(P, PS, xt, r_xt, sq, r_sq, ones, r_ones, gt, r_gt, gcol, eps_ap, rstd, r_rstd, hT, r_hT, TT):
    for half in range(2):
        P.op("act", lambda e: e.activation(out=sq[:, 4 * half:4 * half + 4, :], in_=xt[:, 4 * half:4 * half + 4, :], func=AF.Square),
             reads=[r_xt], writes=[r_sq])
    bank, r_bank = PS.next()
    for k in range(8):
        P.op("pe", lambda e: e.matmul(bank[:, 0:TT], lhsT=ones[:, :], rhs=sq[:, k, :], start=(k == 0), stop=(k == 7)), reads=[r_sq, r_ones], writes=[r_bank])
    P.op("act", lambda e: e.activation(out=rstd[:, :], in_=bank[:, 0:TT], func=AF.Sqrt, bias=eps_ap, scale=1.0 / D), reads=[r_bank, r_gt], writes=[r_rstd])
    P.op("dve", lambda e: e.reciprocal(out=rstd[:, :], in_=rstd[:, :]), reads=[r_rstd], writes=[r_rstd])
    for k in range(8):
        P.op("dve", lambda e: e.scalar_tensor_tensor(out=hT[:, k, :], in0=xt[:, k, :], scalar=gt[:, gcol + k:gcol + k + 1], in1=rstd[:, :], op0=ALU.mult, op1=ALU.mult),
             reads=[r_xt, r_gt, r_rstd], writes=[r_hT])


def _residual_norm(P, PS, mixT, r_mix, xt, r_xt, sq, r_sq, ones, r_ones, gt, r_gt, gcol, eps_ap, rstd, r_rstd, tmp, r_tmp, TT):
    for half in range(2):
        P.op("act", lambda e: e.activation(out=sq[:, 4 * half:4 * half + 4, :], in_=mixT[:, 4 * half:4 * half + 4, :], func=AF.Square),
             reads=[r_mix], writes=[r_sq])
    bank, r_bank = PS.next()
    for k in range(8):
        P.op("pe", lambda e: e.matmul(bank[:, 0:TT], lhsT=ones[:, :], rhs=sq[:, k, :], start=(k == 0), stop=(k == 7)), reads=[r_sq, r_ones], writes=[r_bank])
    P.op("act", lambda e: e.activation(out=rstd[:, :], in_=bank[:, 0:TT], func=AF.Sqrt, bias=eps_ap, scale=1.0 / D), reads=[r_bank, r_gt], writes=[r_rstd])
    P.op("dve", lambda e: e.reciprocal(out=rstd[:, :], in_=rstd[:, :]), reads=[r_rstd], writes=[r_rstd])
    for k in range(8):
        P.op("dve", lambda e: e.scalar_tensor_tensor(out=tmp[:, k, :], in0=mixT[:, k, :], scalar=gt[:, gcol + k:gcol + k + 1], in1=rstd[:, :], op0=ALU.mult, op1=ALU.mult),
             reads=[r_mix, r_gt, r_rstd], writes=[r_tmp])
    for k in range(8):
        P.op("pool", lambda e: e.tensor_tensor(out=xt[:, k, :], in0=xt[:, k, :], in1=tmp[:, k, :], op=ALU.add), reads=[r_tmp, r_xt], writes=[r_xt])


def _load_w(P, wt, r_wt, w_dram, nk, ncols, stage, piece):
    for k in range(nk):
        for c0 in range(0, ncols, piece):
            c1 = min(ncols, c0 + piece)
            st, r_st = stage.next()
            P.op("sp", lambda e: e.dma_start(out=st[:, 0:c1 - c0], in_=w_dram[k * 128:(k + 1) * 128, c0:c1]), writes=[r_st], dma=r_st)
            P.op("pool", lambda e: e.tensor_copy(out=wt[:, k, c0:c1], in_=st[:, 0:c1 - c0]), reads=[r_st], writes=[r_wt])


def build_post1(T):
    TT = 512
    NT = T // TT
    nc = bass.Bass("TRN2", target_bir_lowering=False)
    din = lambda n, s, d=F32: nc.dram_tensor(n, s, d, kind="ExternalInput").ap()
    xT = din("xT", [D, T]); oT = din("oT", [D, T], BF16)
    w_out = din("w_out", [D, D]); w_xq = din("w_xq", [D, D]); w_xkv = din("w_xkv", [D, 2 * D]); w_xo = din("w_xo", [D, D])
    memT = din("memT", [D, MEM]); gains = din("gains", [128, 40])
    xo_d = nc.dram_tensor("xTo", [D, T], F32, kind="ExternalOutput").ap()
    P = Prog(nc)
    PS = PsumRing(P, range(8))
    stage = Ring(P, "wst", 2, [128, 1024], F32)
    wo = P.sb("wo", [128, 8, D], BF16); r_wo = P.res("wo")
    wq = P.sb("wq", [128, 8, D], BF16); r_wq = P.res("wq")
    wxo = P.sb("wxo", [128, 8, D], BF16); r_wxo = P.res("wxo")
    wkv = P.sb("wkv", [128, 8, 2 * D], BF16); r_wkv = P.res("wkv")
    gt = P.sb("gt", [128, 40], F32); r_gt = P.res("gt")
    ones = P.sb("ones", [128, 128], BF16); r_ones = P.res("ones")
    P.op("sp", lambda e: e.dma_start(out=gt[:, :], in_=gains[:, :]), writes=[r_gt], dma=r_gt)
    P.op("pool", lambda e: e.memset(ones[:, :], 1.0), writes=[r_ones])
    eps_ap = gt[:, 32:33]
    _load_w(P, wkv, r_wkv, w_xkv, 8, 2 * D, stage, 1024)
    _load_w(P, wo, r_wo, w_out, 8, D, stage, 1024)
    _load_w(P, wq, r_wq, w_xq, 8, D, stage, 1024)
    _load_w(P, wxo, r_wxo, w_xo, 8, D, stage, 1024)
    xr = Ring(P, "xt", 1, [128, 8, TT], F32)
    otr = Ring(P, "ot", 1, [128, 8, TT], BF16)
    sq = P.sb("sq", [128, 8, TT], BF16); r_sq = P.res("sq")
    hT = P.sb("hT", [128, 8, TT], BF16); r_hT = P.res("hT")
    qT = P.sb("qT", [128, 8, TT], BF16); r_qT = P.res("qT")
    oxT = P.sb("oxT", [128, 8, TT], BF16); r_oxT = P.res("oxT")
    mixT = P.sb("mixT", [128, 8, TT], F32); r_mix = P.res("mixT")
    tmp = P.sb("tmp", [128, 8, TT], F32); r_tmp = P.res("tmp")
    rstd = P.sb("rstd", [128, TT], F32); r_rstd = P.res("rstd")
    rec = P.sb("rec", [128, TT], F32); r_rec = P.res("rec")
    pT = P.sb("pT", [128, 2, TT], BF16); r_pT = P.res("pT")
    kmem = P.sb("kmem", [128, 8, MEM], BF16); r_kmem = P.res("kmem")
    vmem = P.sb("vmem", [128, 2, D], BF16); r_vmem = P.res("vmem")
    memn = P.sb("memn", [128, 8, MEM], BF16); r_memn = P.res("memn")
    mt, r_mt = mixT, r_mix
    P.op("sp", lambda e: e.dma_start(out=mt[:, :, 0:MEM], in_=memT.rearrange("(k p) t -> p k t", p=128)), writes=[r_mt], dma=r_mt)
    P.op("act", lambda e: e.activation(out=sq[:, :, 0:MEM], in_=mt[:, :, 0:MEM], func=AF.Square), reads=[r_mt], writes=[r_sq])
    bank, r_bank = PS.next()
    for k in range(8):
        P.op("pe", lambda e: e.matmul(bank[:, 0:MEM], lhsT=ones[:, :], rhs=sq[:, k, 0:MEM], start=(k == 0), stop=(k == 7)), reads=[r_sq, r_ones], writes=[r_bank])
    P.op("act", lambda e: e.activation(out=rstd[:, 0:MEM], in_=bank[:, 0:MEM], func=AF.Sqrt, bias=eps_ap, scale=1.0 / D), reads=[r_bank, r_gt], writes=[r_rstd])
    P.op("dve", lambda e: e.reciprocal(out=rstd[:, 0:MEM], in_=rstd[:, 0:MEM]), reads=[r_rstd], writes=[r_rstd])
    for k in range(8):
        P.op("dve", lambda e: e.scalar_tensor_tensor(out=memn[:, k, :], in0=mt[:, k, 0:MEM], scalar=gt[:, 16 + k:17 + k], in1=rstd[:, 0:MEM], op0=ALU.mult, op1=ALU.mult),
             reads=[r_mt, r_gt, r_rstd], writes=[r_memn])
    for mc in range(8):
        bank, r_bank = PS.next()
        for k in range(8):
            P.op("pe", lambda e: e.matmul(bank[:, 0:MEM], lhsT=wkv[:, k, mc * 128:(mc + 1) * 128], rhs=memn[:, k, :], start=(k == 0), stop=(k == 7)),
                 reads=[r_wkv, r_memn], writes=[r_bank])
        P.op("act", lambda e: e.activation(out=kmem[:, mc, :], in_=bank[:, 0:MEM], func=AF.Copy), reads=[r_bank], writes=[r_kmem])
    for blk in range(2):
        for half in range(2):
            bank, r_bank = PS.next()
            for k in range(8):
                P.op("pe", lambda e: e.matmul(bank[:, :], lhsT=memn[:, k, blk * 128:(blk + 1) * 128], rhs=wkv[:, k, D + half * 512:D + (half + 1) * 512],
                                              start=(k == 0), stop=(k == 7)), reads=[r_wkv, r_memn], writes=[r_bank])
            P.op("act", lambda e: e.activation(out=vmem[:, blk, half * 512:(half + 1) * 512], in_=bank[:, :], func=AF.Copy), reads=[r_bank], writes=[r_vmem])
    xv = xT.rearrange("(k p) t -> p k t", p=128)
    ov = oT.rearrange("(k p) t -> p k t", p=128)
    xov = xo_d.rearrange("(k p) t -> p k t", p=128)

    def dense(wt, r_wt, src, r_src, evac):
        for mc in range(8):
            bank, r_bank = PS.next()
            for k in range(8):
                P.op("pe", lambda e: e.matmul(bank[:, :], lhsT=wt[:, k, mc * 128:(mc + 1) * 128], rhs=src[:, k, :], start=(k == 0), stop=(k == 7)),
                     reads=[r_wt, r_src], writes=[r_bank])
            evac(mc, bank, r_bank)

    for it in range(NT):
        tok = slice(it * TT, (it + 1) * TT)
        xt, r_xt = xr.next()
        ot, r_ot = otr.next()
        P.op("sp", lambda e: e.dma_start(out=xt[:, :, :], in_=xv[:, :, tok]), writes=[r_xt], dma=r_xt)
        P.op("sp", lambda e: e.dma_start(out=ot[:, :, :], in_=ov[:, :, tok]), writes=[r_ot], dma=r_ot)

        def ev_mix(mc, bank, r_bank):
            eng = "act" if mc % 2 == 0 else "dve"
            if eng == "act":
                P.op("act", lambda e: e.activation(out=mixT[:, mc, :], in_=bank[:, :], func=AF.Copy), reads=[r_bank], writes=[r_mix])
            else:
                P.op("dve", lambda e: e.tensor_copy(out=mixT[:, mc, :], in_=bank[:, :]), reads=[r_bank], writes=[r_mix])
        dense(wo, r_wo, ot, r_ot, ev_mix)
        _residual_norm(P, PS, mixT, r_mix, xt, r_xt, sq, r_sq, ones, r_ones, gt, r_gt, 0, eps_ap, rstd, r_rstd, tmp, r_tmp, TT)
        _prenorm(P, PS, xt, r_xt, sq, r_sq, ones, r_ones, gt, r_gt, 8, eps_ap, rstd, r_rstd, hT, r_hT, TT)

        def ev_q(mc, bank, r_bank):
            P.op("act", lambda e: e.activation(out=qT[:, mc, :], in_=bank[:, :], func=AF.Copy), reads=[r_bank], writes=[r_qT])
        dense(wq, r_wq, hT, r_hT, ev_q)
        for hh in range(4):
            for mb in range(2):
                bank, r_bank = PS.next()
                for cc in range(2):
                    P.op("pe", lambda e: e.matmul(bank[:, :], lhsT=kmem[:, 2 * hh + cc, mb * 128:(mb + 1) * 128], rhs=qT[:, 2 * hh + cc, :], start=(cc == 0), stop=(cc == 1)),
                         reads=[r_kmem, r_qT], writes=[r_bank])
                P.op("act", lambda e: e.activation(out=pT[:, mb, :], in_=bank[:, :], func=AF.Exp, scale=1.0 / 16.0), reads=[r_bank], writes=[r_pT])
            bank, r_bank = PS.next()
            for mb in range(2):
                P.op("pe", lambda e: e.matmul(bank[:, :], lhsT=ones[:, :], rhs=pT[:, mb, :], start=(mb == 0), stop=(mb == 1)), reads=[r_ones, r_pT], writes=[r_bank])
            P.op("dve", lambda e: e.reciprocal(out=rec[:, :], in_=bank[:, :]), reads=[r_bank], writes=[r_rec])
            for cc in range(2):
                bank, r_bank = PS.next()
                for mb in range(2):
                    P.op("pe", lambda e: e.matmul(bank[:, :], lhsT=vmem[:, mb, (2 * hh + cc) * 128:(2 * hh + cc + 1) * 128], rhs=pT[:, mb, :], start=(mb == 0), stop=(mb == 1)),
                         reads=[r_vmem, r_pT], writes=[r_bank])
                P.op("dve", lambda e: e.tensor_tensor(out=oxT[:, 2 * hh + cc, :], in0=bank[:, :], in1=rec[:, :], op=ALU.mult), reads=[r_bank, r_rec], writes=[r_oxT])
        dense(wxo, r_wxo, oxT, r_oxT, ev_mix)
        _residual_norm(P, PS, mixT, r_mix, xt, r_xt, sq, r_sq, ones, r_ones, gt, r_gt, 24, eps_ap, rstd, r_rstd, tmp, r_tmp, TT)
        P.op("sp", lambda e: e.dma_start(out=xov[:, :, tok], in_=xt[:, :, :]), reads=[r_xt], dma=r_xt)
    P.emit()
    return nc


def build_post2(T):
    TT = 256
    NT = T // TT
    NF = D_FF // 128
    nc = bass.Bass("TRN2", target_bir_lowering=False)
    din = lambda n, s, d=F32: nc.dram_tensor(n, s, d, kind="ExternalInput").ap()
    xT = din("xT", [D, T]); w1d = din("w_ffn_in", [D, 2 * D_FF]); w2d = din("w_ffn_out", [D_FF, D]); gains = din("gains", [128, 40])
    xo_d = nc.dram_tensor("xTo", [D, T], F32, kind="ExternalOutput").ap()
    P = Prog(nc)
    PS = PsumRing(P, range(8))
    stage = Ring(P, "wst", 2, [128, 1408], F32)
    w1 = P.sb("w1", [128, 8, 2 * D_FF], BF16); r_w1 = P.res("w1")
    w2 = P.sb("w2", [128, NF, D], BF16); r_w2 = P.res("w2")
    gt = P.sb("gt", [128, 40], F32); r_gt = P.res("gt")
    ones = P.sb("ones", [128, 128], BF16); r_ones = P.res("ones")
    P.op("sp", lambda e: e.dma_start(out=gt[:, :], in_=gains[:, :]), writes=[r_gt], dma=r_gt)
    P.op("pool", lambda e: e.memset(ones[:, :], 1.0), writes=[r_ones])
    eps_ap = gt[:, 32:33]
    _load_w(P, w1, r_w1, w1d, 8, 2 * D_FF, stage, 1408)
    _load_w(P, w2, r_w2, w2d, NF, D, stage, 1024)
    xr = Ring(P, "xt", 2, [128, 8, TT], F32)
    sq = P.sb("sq", [128, 8, TT], BF16); r_sq = P.res("sq")
    hT = P.sb("hT", [128, 8, TT], BF16); r_hT = P.res("hT")
    fT = P.sb("fT", [128, NF, TT], BF16); r_fT = P.res("fT")
    mixT = P.sb("mixT", [128, 8, TT], F32); r_mix = P.res("mixT")
    tmp = P.sb("tmp", [128, 8, TT], F32); r_tmp = P.res("tmp")
    rstd = P.sb("rstd", [128, TT], F32); r_rstd = P.res("rstd")
    sgr = Ring(P, "sg", 2, [128, TT], F32)
    xv = xT.rearrange("(k p) t -> p k t", p=128)
    xov = xo_d.rearrange("(k p) t -> p k t", p=128)
    for it in range(NT):
        tok = slice(it * TT, (it + 1) * TT)
        xt, r_xt = xr.next()
        P.op("sp", lambda e: e.dma_start(out=xt[:, :, :], in_=xv[:, :, tok]), writes=[r_xt], dma=r_xt)
        _prenorm(P, PS, xt, r_xt, sq, r_sq, ones, r_ones, gt, r_gt, 0, eps_ap, rstd, r_rstd, hT, r_hT, TT)
        for j in range(NF):
            bg, r_bg = PS.next()
            bu, r_bu = PS.next()
            for k in range(8):
                P.op("pe", lambda e: e.matmul(bg[:, 0:TT], lhsT=w1[:, k, j * 128:(j + 1) * 128], rhs=hT[:, k, :], start=(k == 0), stop=(k == 7)),
                     reads=[r_w1, r_hT], writes=[r_bg])
            for k in range(8):
                P.op("pe", lambda e: e.matmul(bu[:, 0:TT], lhsT=w1[:, k, D_FF + j * 128:D_FF + (j + 1) * 128], rhs=hT[:, k, :], start=(k == 0), stop=(k == 7)),
                     reads=[r_w1, r_hT], writes=[r_bu])
            sg, r_sg = sgr.next()
            P.op("act", lambda e: e.activation(out=sg[:, :], in_=bg[:, 0:TT], func=AF.Silu), reads=[r_bg], writes=[r_sg])
            P.op("dve", lambda e: e.tensor_tensor(out=fT[:, j, :], in0=sg[:, :], in1=bu[:, 0:TT], op=ALU.mult), reads=[r_sg, r_bu], writes=[r_fT])
        for mc in range(8):
            bank, r_bank = PS.next()
            for j in range(NF):
                P.op("pe", lambda e: e.matmul(bank[:, 0:TT], lhsT=w2[:, j, mc * 128:(mc + 1) * 128], rhs=fT[:, j, :], start=(j == 0), stop=(j == NF - 1)),
                     reads=[r_w2, r_fT], writes=[r_bank])
            P.op("act", lambda e: e.activation(out=mixT[:, mc, :], in_=bank[:, 0:TT], func=AF.Copy), reads=[r_bank], writes=[r_mix])
        _residual_norm(P, PS, mixT, r_mix, xt, r_xt, sq, r_sq, ones, r_ones, gt, r_gt, 8, eps_ap, rstd, r_rstd, tmp, r_tmp, TT)
        P.op("sp", lambda e: e.dma_start(out=xov[:, :, tok], in_=xt[:, :, :]), reads=[r_xt], dma=r_xt)
    P.emit()
    return nc


_PROGS = {}


def _prog(key, fn):
    if key not in _PROGS:
        _PROGS[key] = fn()
    return _PROGS[key]


def _run(nc, in_maps):
    res = run_bass_kernel_spmd(nc, in_maps, core_ids=list(range(NCORES)))
    return res.results


def _gl(g):
    return np.ascontiguousarray(np.asarray(g, np.float32).reshape(-1, 128).T)


def kernel(x, mem, positions, g_mix_pre, g_mix_post, w_in, diff_lambda, diff_head_g, mla_q_norm_g, mla_w_q_up,
           mla_kv_norm_g, mla_w_kv_up, swa_sinks, w_out, g_x_pre, g_x_mem, g_x_post, w_xq, w_xkv, w_xo,
           g_ffn_pre, g_ffn_post, w_ffn_in, w_ffn_out):
    f32 = lambda a: np.ascontiguousarray(np.asarray(a, np.float32))
    x = f32(x); mem = f32(mem); positions = np.ascontiguousarray(np.asarray(positions, np.int32))
    B, S, _ = x.shape
    T = S // 4
    TH = T + 256
    cores = [(b, r) for b in range(B) for r in range(4)]
    cst, rm, ident, m_prev, m_next = make_consts()
    TA = 1024
    aux = _prog(("aux", TA), lambda: build_aux(TA))
    kaug_p = [[] for _ in cores]; kaug_n = [[] for _ in cores]; qaug = [[] for _ in cores]; rcos = [[] for _ in cores]; rsin = [[] for _ in cores]
    for j in range(T // TA):
        res = _run(aux, [{"pos": np.ascontiguousarray(positions[b, r * T + j * TA:r * T + (j + 1) * TA][None, :]), "cst": cst} for (b, r) in cores])
        for c in range(NCORES):
            kaug_p[c].append(np.asarray(res[c]["kaug"])); kaug_n[c].append(np.asarray(res[c]["kaugn"]))
            qaug[c].append(np.asarray(res[c]["qaug"])); rcos[c].append(np.asarray(res[c]["rcos"])); rsin[c].append(np.asarray(res[c]["rsin"]))
    cat = lambda lst, ax: np.ascontiguousarray(np.concatenate(lst, axis=ax))
    kaug_p = [cat(a, 1) for a in kaug_p]; kaug_n = [cat(a, 1) for a in kaug_n]; qaug = [cat(a, 3) for a in qaug]
    rcos = [cat(a, 1) for a in rcos]; rsin = [cat(a, 1) for a in rsin]
    xT = [np.ascontiguousarray(x[b, r * T:(r + 1) * T, :].T) for (b, r) in cores]
    memT = [np.ascontiguousarray(mem[b].T) for b in range(B)]
    p_proj = _prog(("proj", T), lambda: build_proj(T))
    p_att = [_prog(("att", T, fl), lambda fl=fl: build_att(T, *fl)) for fl in ((True, False, False), (False, True, False), (False, False, True))]
    p_post1 = _prog(("post1", T), lambda: build_post1(T))
    p_post2 = _prog(("post2", T), lambda: build_post2(T))
    depth = np.asarray(w_in).shape[0]
    for l in range(depth):
        gains = np.zeros((128, 16), np.float32)
        gains[:, 0:8] = _gl(g_mix_pre[l]); gains[:, 8:10] = _gl(mla_q_norm_g[l]); gains[:, 10:11] = _gl(mla_kv_norm_g[l]); gains[:, 11] = EPS
        wl = {"w_in": f32(w_in[l]), "w_qup": f32(mla_w_q_up[l]), "w_kvup": f32(mla_w_kv_up[l]), "gains": gains, "rm": rm}
        pr = _run(p_proj, [dict(wl, xT=xT[c], rcos=rcos[c], rsin=rsin[c]) for c in range(NCORES)])
        pr = [{k: np.asarray(v) for k, v in d.items()} for d in pr]
        lam_init = 0.8 - 0.6 * math.exp(-0.3 * l)
        dhg = np.zeros((64, 4), np.float32)
        dhg[:, 0] = np.asarray(diff_head_g[l], np.float32); dhg[:, 1] = EPS; dhg[:, 2] = -lam_init; dhg[:, 3] = 1.0 - lam_init
        sinks = np.ascontiguousarray(np.tile(np.asarray(swa_sinks[l], np.float32)[None, :], (128, 1)))
        dlam = f32(diff_lambda[l]).reshape(1, 128)
        att_in = []
        for ci, (b, r) in enumerate(cores):
            order = [r] + [q for q in range(4) if q != r]
            cc = lambda q: b * 4 + q
            gat1 = lambda key: cat([pr[cc(q)][key] for q in order], 1)
            gat0 = lambda key: cat([pr[cc(q)][key] for q in order], 0)
            kaug_all = cat([kaug_p[cc(q)] if q >= r else kaug_n[cc(q)] for q in order], 1)

            def halo_cols(get):
                own = get(ci)
                lo = get(cc(r - 1))[:, -128:] if r > 0 else np.zeros_like(own[:, :128])
                hi = get(cc(r + 1))[:, :128] if r < 3 else np.zeros_like(own[:, :128])
                return cat([lo, own, hi], 1)
            vs_own = lambda c: np.ascontiguousarray(pr[c]["vds"][:, 256:384].T)
            masks = np.stack([np.tile(m_prev, (1, 4)), np.tile(m_next, (1, 4)),
                              np.tile(m_prev, (1, 4)) * (1.0 if r > 0 else 0.0), np.tile(m_next, (1, 4)) * (1.0 if r < 3 else 0.0)]).astype(np.float32)
            common = {"sinks": sinks, "dlam": dlam, "dhg": dhg, "masks": masks}
            att_in.append((
                dict(common, qd=pr[ci]["qd"], qaug=qaug[ci], kd=gat1("kd"), kaug=kaug_all,
                     vds=cat([np.ascontiguousarray(pr[cc(q)]["vds"][:, 0:256]) for q in order], 0)),
                dict(common, qm=pr[ci]["qm"], kn=gat1("kn"), kr=gat1("kr"), vm=gat0("vm")),
                dict(common, qs=pr[ci]["qs"], qaug=qaug[ci], ksl=halo_cols(lambda c: pr[c]["ks"]), kaugl=halo_cols(lambda c: kaug_p[c]),
                     vsl=np.ascontiguousarray(halo_cols(vs_own).T)),
            ))
        oT = [None] * NCORES
        parts = []
        for k, pa in enumerate(p_att):
            ra = _run(pa, [att_in[c][k] for c in range(NCORES)])
            parts.append([np.asarray(ra[c]["oT"]) for c in range(NCORES)])
        for c in range(NCORES):
            oT[c] = cat([parts[0][c], parts[1][c], parts[2][c]], 0)
        del att_in, parts, pr
        gains1 = np.zeros((128, 40), np.float32)
        gains1[:, 0:8] = _gl(g_mix_post[l]); gains1[:, 8:16] = _gl(g_x_pre[l]); gains1[:, 16:24] = _gl(g_x_mem[l]); gains1[:, 24:32] = _gl(g_x_post[l]); gains1[:, 32] = EPS
        w1l = {"w_out": f32(w_out[l]), "w_xq": f32(w_xq[l]), "w_xkv": f32(w_xkv[l]), "w_xo": f32(w_xo[l]), "gains": gains1}
        r1 = _run(p_post1, [dict(w1l, xT=xT[c], oT=oT[c], memT=memT[cores[c][0]]) for c in range(NCORES)])
        xT = [np.asarray(r1[c]["xTo"]) for c in range(NCORES)]
        gains2 = np.zeros((128, 40), np.float32)
        gains2[:, 0:8] = _gl(g_ffn_pre[l]); gains2[:, 8:16] = _gl(g_ffn_post[l]); gains2[:, 32] = EPS
        w2l = {"w_ffn_in": f32(w_ffn_in[l]), "w_ffn_out": f32(w_ffn_out[l]), "gains": gains2}
        r2 = _run(p_post2, [dict(w2l, xT=xT[c]) for c in range(NCORES)])
        xT = [np.asarray(r2[c]["xTo"]) for c in range(NCORES)]
    out = np.empty((B, S, D), np.float32)
    for c, (b, r) in enumerate(cores):
        out[b, r * T:(r + 1) * T, :] = xT[c].T
    return out
```
